# Optimizing a Trainium2 kernel written in Bass

```python
import math
import jax, jax.numpy as jnp
from jax import lax
import numpy as np

D_MODEL = 1024
BATCH = 8
SEQ = 2048
DEPTH = 1

PLE_DIM = 256
M_HEADS = 4
M_HEAD_DIM = 128
M_WIDTH = M_HEADS * M_HEAD_DIM
M_CONV = 4
M_CHUNK = 64
DA_HEADS = 4
DA_HEAD_DIM = 64
DA_V_DIM = 2 * DA_HEAD_DIM
DA_QK_WIDTH = DA_HEADS * 2 * DA_HEAD_DIM
DA_WIDTH = DA_HEADS * DA_V_DIM
Q_BLOCK = 128
REL_BUCKETS = 32
REL_MAX_DIST = 128
D_FF = 2816
FFN_CONV = 3
N_BRANCHES = 2
EPS = 1e-6
NEG_BIG = -1e30

SPLITS = (M_WIDTH, M_WIDTH, M_WIDTH, M_WIDTH, M_HEADS, M_HEADS,
          DA_QK_WIDTH, DA_QK_WIDTH, DA_WIDTH, N_BRANCHES * D_MODEL)
IN_COLS = sum(SPLITS)

kernel_name = "hybrid_mlstm_diffattn_convffn_block"


def rmsnorm(x, g):
    xf = x.astype(jnp.float32)
    y = xf * lax.rsqrt(jnp.mean(xf * xf, -1, keepdims=True) + EPS)
    return (y * g.astype(jnp.float32)).astype(x.dtype)


def head_rmsnorm(x, g):
    H, d = x.shape[-2], x.shape[-1]
    xf = x.astype(jnp.float32)
    y = xf * lax.rsqrt(jnp.mean(xf * xf, -1, keepdims=True) + EPS)
    return y * g.reshape(H, d).astype(jnp.float32)


def causal_dwconv(x, w, b):
    K, C = w.shape
    y = lax.conv_general_dilated(x, w[:, None, :].astype(x.dtype), window_strides=(1,),
                                 padding=[(K - 1, 0)],
                                 dimension_numbers=("NWC", "WIO", "NWC"),
                                 feature_group_count=C)
    return y + b.astype(x.dtype)


def t5_causal_bucket(q_pos, k_pos):
    n = jnp.maximum(q_pos[:, None] - k_pos[None, :], 0)
    max_exact = REL_BUCKETS // 2
    nf = jnp.maximum(n, 1).astype(jnp.float32)
    large = max_exact + (jnp.log(nf / max_exact) / math.log(REL_MAX_DIST / max_exact)
                         * (REL_BUCKETS - max_exact)).astype(jnp.int32)
    large = jnp.minimum(large, REL_BUCKETS - 1)
    return jnp.where(n < max_exact, n, large)


def mlstm_chunkwise(q, k, v, i_pre, f_pre):
    Bsz, H, S, d = q.shape
    L = M_CHUNK
    NC = S // L
    f32 = jnp.float32
    qc = q.astype(f32).reshape(Bsz, H, NC, L, d)
    kc = (k.astype(f32) * (d ** -0.5)).reshape(Bsz, H, NC, L, d)
    vc = v.astype(f32).reshape(Bsz, H, NC, L, d)
    logf = jax.nn.log_sigmoid(f_pre).reshape(Bsz, H, NC, L)
    ig = i_pre.reshape(Bsz, H, NC, L)
    b = jnp.cumsum(logf, -1)
    b_tot = b[..., -1]
    causal = jnp.tril(jnp.ones((L, L), bool))
    log_d = jnp.where(causal, b[..., :, None] - b[..., None, :] + ig[..., None, :], -jnp.inf)

    w_end = b_tot[..., None] - b + ig
    m_loc = jnp.max(w_end, -1)
    a = jnp.exp(w_end - m_loc[..., None])
    ak = a[..., None] * kc
    C_loc = jnp.einsum('bhcld,bhcle->bhcde', ak, vc)
    n_loc = jnp.sum(ak, axis=3)

    def step(carry, inp):
        C, n, m = carry
        C_l, n_l, m_l, bt = inp
        m_new = jnp.maximum(bt + m, m_l)
        s_old = jnp.exp(bt + m - m_new)
        s_loc = jnp.exp(m_l - m_new)
        C_new = s_old[..., None, None] * C + s_loc[..., None, None] * C_l
        n_new = s_old[..., None] * n + s_loc[..., None] * n_l
        return (C_new, n_new, m_new), (C, n, m)

    init = (jnp.zeros((Bsz, H, d, d), f32), jnp.zeros((Bsz, H, d), f32), jnp.zeros((Bsz, H), f32))
    xs = (jnp.moveaxis(C_loc, 2, 0), jnp.moveaxis(n_loc, 2, 0),
          jnp.moveaxis(m_loc, 2, 0), jnp.moveaxis(b_tot, 2, 0))
    _, (C_prev, n_prev, m_prev) = lax.scan(step, init, xs)
    C_prev = jnp.moveaxis(C_prev, 0, 2)
    n_prev = jnp.moveaxis(n_prev, 0, 2)
    m_prev = jnp.moveaxis(m_prev, 0, 2)

    log_inter = b + m_prev[..., None]
    m_t = jnp.maximum(log_inter, jnp.max(log_d, -1))
    s_inter = jnp.exp(log_inter - m_t)
    dmat = jnp.exp(log_d - m_t[..., None])
    sqk = jnp.einsum('bhcld,bhcsd->bhcls', qc, kc) * dmat
    num = (s_inter[..., None] * jnp.einsum('bhcld,bhcde->bhcle', qc, C_prev)
           + jnp.einsum('bhcls,bhcse->bhcle', sqk, vc))
    den = s_inter * jnp.einsum('bhcld,bhcd->bhcl', qc, n_prev) + jnp.sum(sqk, -1)
    h = num / jnp.maximum(jnp.abs(den), jnp.exp(-m_t))[..., None]
    return h.reshape(Bsz, H, S, d)


def diff_attention(q1, q2, k1, k2, v, lam, rel_bias):
    Bsz, H, S, dh = q1.shape
    NB = S // Q_BLOCK
    scale = DA_HEAD_DIM ** -0.5
    k_pos = jnp.arange(S)

    def to_blocks(t):
        return jnp.moveaxis(t.reshape(Bsz, H, NB, Q_BLOCK, dh), 2, 0)

    def block(args):
        qb1, qb2, q_start = args
        q_pos = q_start + jnp.arange(Q_BLOCK)
        bucket = t5_causal_bucket(q_pos, k_pos)
        bias = jnp.moveaxis(rel_bias[bucket], -1, 0).astype(jnp.float32)
        mask = q_pos[:, None] >= k_pos[None, :]

        def probs(qb, kk):
            s = jnp.einsum('bhqd,bhkd->bhqk', qb, kk, preferred_element_type=jnp.float32) * scale + bias
            return jax.nn.softmax(jnp.where(mask, s, NEG_BIG), axis=-1)

        a = probs(qb1, k1) - lam * probs(qb2, k2)
        return jnp.einsum('bhqk,bhkd->bhqd', a.astype(v.dtype), v)

    out = lax.map(block, (to_blocks(q1), to_blocks(q2), jnp.arange(NB) * Q_BLOCK))
    return jnp.moveaxis(out, 0, 2).reshape(Bsz, H, S, v.shape[-1])


def setup_inputs(seed: int = 0) -> dict:
    key = jax.random.key(seed)
    ks = jax.random.split(key, 24)
    f32 = jnp.float32
    nrm = lambda k, shape, s: jax.random.normal(k, shape, f32) * s
    b_i = nrm(ks[4], (DEPTH, M_HEADS), 0.1)
    b_f = jnp.broadcast_to(jnp.linspace(3.0, 6.0, M_HEADS, dtype=f32), (DEPTH, M_HEADS)) + nrm(ks[5], (DEPTH, M_HEADS), 0.05)
    return {
        "x": nrm(ks[0], (BATCH, SEQ, D_MODEL), 1.0),
        "p": nrm(ks[1], (DEPTH, BATCH, SEQ, PLE_DIM), 1.0),
        "rel_bias": nrm(ks[2], (REL_BUCKETS, DA_HEADS), 0.2),
        "norm_mix_g": 1.0 + nrm(ks[3], (DEPTH, D_MODEL), 0.02),
        "w_in": nrm(ks[6], (DEPTH, D_MODEL, IN_COLS), D_MODEL ** -0.5),
        "b_if": jnp.concatenate([b_i, b_f], axis=-1),
        "m_conv_w": nrm(ks[7], (DEPTH, M_CONV, 2 * M_WIDTH), M_CONV ** -0.5),
        "m_conv_b": nrm(ks[8], (DEPTH, 2 * M_WIDTH), 0.01),
        "m_norm_g": 1.0 + nrm(ks[9], (DEPTH, M_WIDTH), 0.02),
        "da_lambda": nrm(ks[10], (DEPTH, 4, DA_HEAD_DIM), 0.1),
        "da_norm_g": 1.0 + nrm(ks[11], (DEPTH, DA_WIDTH), 0.02),
        "w_br_m": nrm(ks[12], (DEPTH, M_WIDTH, D_MODEL), M_WIDTH ** -0.5),
        "w_br_d": nrm(ks[13], (DEPTH, DA_WIDTH, D_MODEL), DA_WIDTH ** -0.5),
        "w_out": nrm(ks[14], (DEPTH, D_MODEL, D_MODEL), D_MODEL ** -0.5),
        "norm_ffn_g": 1.0 + nrm(ks[15], (DEPTH, D_MODEL), 0.02),
        "w_up": nrm(ks[16], (DEPTH, D_MODEL, 2 * D_FF), D_MODEL ** -0.5),
        "ffn_conv_w": nrm(ks[17], (DEPTH, FFN_CONV, 2 * D_FF), FFN_CONV ** -0.5),
        "ffn_conv_b": nrm(ks[18], (DEPTH, 2 * D_FF), 0.01),
        "w_down": nrm(ks[19], (DEPTH, D_FF, D_MODEL), D_FF ** -0.5),
        "norm_ple_g": 1.0 + nrm(ks[20], (DEPTH, D_MODEL), 0.02),
        "w_ple_gate": nrm(ks[21], (DEPTH, D_MODEL, D_MODEL), D_MODEL ** -0.5),
        "w_ple": nrm(ks[22], (DEPTH, PLE_DIM, D_MODEL), PLE_DIM ** -0.5),
        "norm_final_g": 1.0 + nrm(ks[23], (D_MODEL,), 0.02),
    }


def reference(x, p, rel_bias, norm_mix_g, w_in, b_if, m_conv_w, m_conv_b, m_norm_g,
              da_lambda, da_norm_g, w_br_m, w_br_d, w_out, norm_ffn_g, w_up, ffn_conv_w,
              ffn_conv_b, w_down, norm_ple_g, w_ple_gate, w_ple, norm_final_g):
    Bsz, S, _ = x.shape
    f32 = jnp.float32
    split_idx = [int(c) for c in np.cumsum(SPLITS)[:-1]]

    def heads(t, H):
        return t.reshape(Bsz, S, H, -1).transpose(0, 2, 1, 3)

    for l in range(DEPTH):
        h = rmsnorm(x, norm_mix_g[l])
        proj = h @ w_in[l]
        mq, mk, mv, mo, mi, mf, dq, dk, dv, gates = jnp.split(proj, split_idx, axis=-1)

        qk = jax.nn.silu(causal_dwconv(jnp.concatenate([mq, mk], -1), m_conv_w[l], m_conv_b[l]))
        mq_c, mk_c = jnp.split(qk, 2, axis=-1)
        i_pre = (mi.astype(f32) + b_if[l, :M_HEADS].astype(f32)).transpose(0, 2, 1)
        f_pre = (mf.astype(f32) + b_if[l, M_HEADS:].astype(f32)).transpose(0, 2, 1)
        hm = mlstm_chunkwise(heads(mq_c, M_HEADS), heads(mk_c, M_HEADS), heads(mv, M_HEADS), i_pre, f_pre)
        hm = head_rmsnorm(hm.transpose(0, 2, 1, 3), m_norm_g[l]).reshape(Bsz, S, M_WIDTH).astype(x.dtype)
        ya = (jax.nn.sigmoid(mo) * hm) @ w_br_m[l]

        dq5 = dq.reshape(Bsz, S, DA_HEADS, 2, DA_HEAD_DIM)
        dk5 = dk.reshape(Bsz, S, DA_HEADS, 2, DA_HEAD_DIM)
        q1, q2 = dq5[..., 0, :].transpose(0, 2, 1, 3), dq5[..., 1, :].transpose(0, 2, 1, 3)
        k1, k2 = dk5[..., 0, :].transpose(0, 2, 1, 3), dk5[..., 1, :].transpose(0, 2, 1, 3)
        lam_init = 0.8 - 0.6 * math.exp(-0.3 * l)
        lv = da_lambda[l].astype(f32)
        lam = jnp.exp(jnp.sum(lv[0] * lv[1])) - jnp.exp(jnp.sum(lv[2] * lv[3])) + lam_init
        hd = diff_attention(q1, q2, k1, k2, heads(dv, DA_HEADS), lam, rel_bias)
        hd = head_rmsnorm(hd.transpose(0, 2, 1, 3), da_norm_g[l]) * (1.0 - lam_init)
        yb = hd.reshape(Bsz, S, DA_WIDTH).astype(x.dtype) @ w_br_d[l]

        g_a, g_b = jnp.split(gates, 2, axis=-1)
        mixed = jax.nn.sigmoid(g_a) * ya + jax.nn.sigmoid(g_b) * yb
        x = x + mixed @ w_out[l]

        h = rmsnorm(x, norm_ffn_g[l])
        u = causal_dwconv(h @ w_up[l], ffn_conv_w[l], ffn_conv_b[l])
        val, gt = jnp.split(u, 2, axis=-1)
        x = x + (jax.nn.gelu(gt) * val) @ w_down[l]

        hg = rmsnorm(x, norm_ple_g[l])
        x = x + jax.nn.sigmoid(hg @ w_ple_gate[l]) * (p[l].astype(x.dtype) @ w_ple[l])

    return rmsnorm(x, norm_final_g)
```

```python
import math
from contextlib import ExitStack

import numpy as np
import concourse.bass as bass
import concourse.mybir as mybir
from concourse.bass_utils import run_bass_kernel_spmd

F32 = mybir.dt.float32
BF16 = mybir.dt.bfloat16
AF = mybir.ActivationFunctionType
ALU = mybir.AluOpType
AX = mybir.AxisListType

D = 1024
T = 2048
NT = 16
NQ = 4
DFF = 2816
NJ = 22
IN_COLS = 5640
C_MQ, C_MK, C_MV, C_MO, C_MI, C_DQ, C_DK, C_DV, C_GA, C_GB = 0, 512, 1024, 1536, 2048, 2056, 2568, 3080, 3592, 4616
EPS = 1e-6
NEG = -30000.0
LNK = math.log(128.0 ** -0.5)
LAM_INIT = 0.8 - 0.6 * math.exp(-0.3 * 0)

CS = {}
_o = 0
for _n, _w in (("g_mix", 8), ("g_ffn", 8), ("g_ple", 8), ("g_fin", 8), ("mcw", 32), ("mcb", 8),
               ("fcw", 132), ("fcb", 44), ("bif", 8), ("c31", 4), ("dal", 256)):
    CS[_n] = _o
    _o += _w
CS_N = _o
KC = {}
_o = 0
for _n, _w in (("maskA", 640), ("tri", 128), ("maskT", 128), ("maskL", 128), ("ident", 128), ("eps", 1), ("lnk", 1)):
    KC[_n] = _o
    _o += _w
KC_N = _o

SAME_ENGINE_SYNC = True


class Buf:
    __slots__ = ("name", "w", "r")

    def __init__(self, name):
        self.name = name
        self.w = None
        self.r = []


class Sched:
    ENGS = ("pe", "act", "dve", "pool", "sp")

    def __init__(self, nc, stack):
        self.nc = nc
        self.stack = stack
        self.semobj = {}
        for k in self.ENGS:
            self.semobj[k] = stack.enter_context(nc.semaphore("s_" + k))
        self.cnt = {k: 0 for k in self.ENGS}
        self.seen = {k: {} for k in self.ENGS}
        self.prog = {k: [] for k in self.ENGS}
        self.dcnt = {}
        self.nops = 0
        self.cut = 10 ** 12

    def _wait(self, e, tok):
        key, val = tok
        if key == e and (e == "pe" or not SAME_ENGINE_SYNC):
            return
        if self.seen[e].get(key, 0) >= val:
            return
        self.seen[e][key] = val
        sem = self.semobj[key]
        self.prog[e].append(lambda q, sem=sem, val=val: q.wait_ge(sem, val))

    def _deps(self, e, reads, writes):
        for b in reads:
            if b.w is not None:
                self._wait(e, b.w)
        for b in writes:
            if b.w is not None:
                self._wait(e, b.w)
            for t in b.r:
                self._wait(e, t)

    def op(self, e, fn, reads=(), writes=(), inc=True):
        self.nops += 1
        if self.nops > self.cut:
            return None
        self._deps(e, reads, writes)
        idx = self.cnt[e] + 1
        tok = (e, idx)
        if inc:
            self.cnt[e] = idx
            sem = self.semobj[e]
            self.prog[e].append(lambda q, fn=fn, sem=sem: fn(q).then_inc(sem, 1))
        else:
            self.prog[e].append(lambda q, fn=fn: fn(q))
        for b in reads:
            b.r.append(tok)
            if len(b.r) > 24:
                b.r = _compact(b.r)
        for b in writes:
            b.w = tok
            b.r = []
        return tok

    def dma(self, e, out_ap, in_ap, semname, reads=(), writes=()):
        if semname not in self.semobj:
            self.semobj[semname] = self.stack.enter_context(self.nc.semaphore("d_" + semname))
            self.dcnt[semname] = 0
        self._deps(e, reads, writes)
        self.dcnt[semname] += 16
        tok = (semname, self.dcnt[semname])
        sem = self.semobj[semname]
        self.prog[e].append(
            lambda q, o=out_ap, i=in_ap, sem=sem: q.dma_start(out=o, in_=i).then_inc(sem, 16))
        for b in reads:
            b.r.append(tok)
        for b in writes:
            b.w = tok
            b.r = []
        return tok

    def barrier(self):
        for e in self.ENGS:
            for f in self.ENGS:
                if f != e and self.cnt[f] > 0:
                    self._wait(e, (f, self.cnt[f]))
            for name, c in self.dcnt.items():
                if c > 0:
                    self._wait(e, (name, c))

    def emit(self):
        with self.nc.Block() as blk:
            for name, deco in (("pe", blk.tensor), ("act", blk.scalar), ("dve", blk.vector),
                               ("pool", blk.gpsimd), ("sp", blk.sync)):
                lst = self.prog[name]

                def f(q, lst=lst):
                    for c in lst:
                        c(q)
                deco(f)
                self.prog[name] = []


def _compact(toks):
    best = {}
    for k, v in toks:
        if best.get(k, 0) < v:
            best[k] = v
    return list(best.items())


class Region:
    def __init__(self, arena, start, end):
        self.arena, self.start, self.end, self.p = arena, start, end, start

    def reset(self):
        self.p = self.start

    def alloc(self, shape, dt):
        n = 1
        for s in shape:
            n *= s
        nbytes = n * (4 if dt == F32 else 2)
        nbytes = (nbytes + 63) // 64 * 64
        if self.p + nbytes > self.end:
            return None
        a = self.p // 4
        b = a + nbytes // 4
        self.p += nbytes
        ap = self.arena[:, a:b]
        if dt == BF16:
            ap = ap.bitcast(BF16)
        ap = ap[:, 0:n]
        if len(shape) == 2:
            ap = ap.rearrange("p (a b) -> p a b", a=shape[0])
        elif len(shape) == 3:
            ap = ap.rearrange("p (a b c) -> p a b c", a=shape[0], b=shape[1])
        return ap


class Alloc:
    def __init__(self, regions):
        self.regions = regions

    def reset(self, regions):
        self.regions = regions
        for r in regions:
            r.reset()

    def __call__(self, shape, dt):
        for r in self.regions:
            ap = r.alloc(shape, dt)
            if ap is not None:
                return ap
        raise RuntimeError("SBUF arena exhausted for %s" % (shape,))


def build_nc(stop_after=None, dumps=()):
    nc = bass.Bass("TRN2", target_bir_lowering=False)
    dram = lambda name, shape, dt=F32, kind="ExternalInput": nc.dram_tensor(name, list(shape), dt, kind=kind).ap()
    xT_d = dram("xT", [D, T])
    pT_d = dram("pT", [256, T])
    w_in = dram("w_in", [D, IN_COLS])
    w_br_m = dram("w_br_m", [512, D])
    w_br_d = dram("w_br_d", [512, D])
    w_out = dram("w_out", [D, D])
    w_up = dram("w_up", [D, 2 * DFF])
    w_down = dram("w_down", [DFF, D])
    w_pg = dram("w_ple_gate", [D, D])
    w_pl = dram("w_ple", [256, D])
    cs_d = dram("cs", [128, CS_N])
    kc_d = dram("kc", [128, KC_N])
    rows_d = dram("rows", [128, 1024])
    biasT_d = dram("biasT", [128, 4 * 640])
    outT_d = dram("outT", [D, T], F32, "ExternalOutput")
    dump_d = {}

    with ExitStack() as top:
        S = Sched(nc, top)
        NA = 53000
        arena = top.enter_context(nc.sbuf_tensor("arena", [128, NA], F32))
        ps = [top.enter_context(nc.psum_tensor("ps%d" % i, [128, 512], F32)) for i in range(8)]
        KB = 1024
        RP = Region(arena, 0, 40 * KB)
        RX = Region(arena, 40 * KB, 104 * KB)
        RY = Region(arena, 104 * KB, 136 * KB)
        RZ = Region(arena, 136 * KB, NA * 4)
        cs = RP.alloc([CS_N], F32)
        kc = RP.alloc([KC_N], F32)
        ones = RP.alloc([128], F32)
        identb = RP.alloc([128], BF16)
        hT = RP.alloc([8, T], BF16)
        assert hT is not None
        B_cs, B_kc, B_ones, B_identb, B_hT = Buf("cs"), Buf("kc"), Buf("ones"), Buf("identb"), Buf("hT")
        ident = kc[:, KC["ident"]:KC["ident"] + 128]
        tri = kc[:, KC["tri"]:KC["tri"] + 128]
        maskT = kc[:, KC["maskT"]:KC["maskT"] + 128]
        maskL = kc[:, KC["maskL"]:KC["maskL"] + 128]
        maskA = kc[:, KC["maskA"]:KC["maskA"] + 640]
        epsc = kc[:, KC["eps"]:KC["eps"] + 1]
        lnkc = kc[:, KC["lnk"]:KC["lnk"] + 1]
        csc = lambda name, i=0, n=1: cs[:, CS[name] + i:CS[name] + i + n]
        Bps = [Buf("ps%d" % i) for i in range(8)]

        def tsl(t, n=128):
            return slice(t * n, (t + 1) * n)

        def wview(w_ap, c0, n):
            return w_ap[:, c0:c0 + n].rearrange("(kc p) n -> p kc n", p=128)

        def dump(name, ap, bufs, shape, dt=F32):
            if name not in dumps:
                return
            d = nc.dram_tensor("dbg_" + name, list(shape), dt, kind="ExternalOutput").ap()
            dump_d[name] = d
            S.dma("sp", d, ap, "dbg_" + name, reads=bufs)

        def phase_end(name):
            S.barrier()
            S.emit()
            return stop_after == name

        def rmsnorm(src, gname, dst, A, psbank=0):
            sq = [A([512], F32) for _ in range(2)]
            Bsq = [Buf("sq0"), Buf("sq1")]
            rs = [A([512], F32) for _ in range(2)]
            Brs = [Buf("rs0"), Buf("rs1")]
            for tt in range(NQ):
                pb = psbank + (tt % 2)
                for c in range(8):
                    sap, sb = src(tt, c)
                    k = c % 2
                    S.op("act", lambda q, o=sq[k], i=sap: q.activation(out=o, in_=i, func=AF.Square),
                         reads=sb, writes=[Bsq[k]])
                    S.op("pe", lambda q, o=ps[pb], r=sq[k], c=c: q.matmul(o[:], lhsT=ones, rhs=r, start=(c == 0), stop=(c == 7)),
                         reads=[Bsq[k], B_ones], writes=[Bps[pb]])
                r = rs[tt % 2]
                S.op("act", lambda q, o=r, i=ps[pb]: q.activation(out=o, in_=i[:], func=AF.Ln, scale=1.0 / D, bias=epsc),
                     reads=[Bps[pb], B_kc], writes=[Brs[tt % 2]])
                S.op("act", lambda q, o=r: q.activation(out=o, in_=o, func=AF.Exp, scale=-0.5), reads=[Brs[tt % 2]], writes=[Brs[tt % 2]])
                for c in range(8):
                    sap, sb = src(tt, c)
                    dap, db, after = dst(tt, c)
                    S.op("dve", lambda q, o=dap, i=sap, g=csc(gname, c), r=r: q.scalar_tensor_tensor(
                        out=o, in0=i, scalar=g, in1=r, op0=ALU.mult, op1=ALU.mult),
                        reads=sb + [Brs[tt % 2], B_cs], writes=db)
                    if after is not None:
                        after()

        A = Alloc([RX, RZ])
        S.dma("sp", cs, cs_d, "cs", writes=[B_cs])
        S.dma("sp", kc, kc_d, "kc", writes=[B_kc])
        S.op("dve", lambda q: q.memset(ones, 1.0), writes=[B_ones])
        S.op("dve", lambda q: q.tensor_copy(out=identb, in_=ident), reads=[B_kc], writes=[B_identb])
        xs = [A([8, 512], F32) for _ in range(2)]
        Bxs = [Buf("xs0"), Buf("xs1")]
        for tt in range(NQ):
            pass

        def srcA(tt, c):
            return xs[tt % 2][:, c, :], [Bxs[tt % 2]]

        def dstA(tt, c):
            return hT[:, c, tsl(tt, 512)], [B_hT], None

        _loaded = set()

        def srcA_load(tt, c):
            if tt not in _loaded:
                _loaded.add(tt)
                S.dma("sp", xs[tt % 2], xT_d[:, tsl(tt, 512)].rearrange("(c p) t -> p c t", p=128),
                      "xs%d" % (tt % 2), writes=[Bxs[tt % 2]])
                if tt + 1 < NQ and tt == 0:
                    _loaded.add(1)
                    S.dma("sp", xs[1], xT_d[:, tsl(1, 512)].rearrange("(c p) t -> p c t", p=128),
                          "xs1", writes=[Bxs[1]])
            return srcA(tt, c)

        rmsnorm(srcA_load, "g_mix", dstA, A, psbank=0)
        dump("hT", hT, [B_hT], [128, 8, T], BF16)
        if phase_end("A"):
            return nc, dump_d

        A.reset([RX, RZ])
        RY.reset()
        gmT = RY.alloc([4, T], BF16)
        gdT = RY.alloc([4, T], BF16)
        B_gmT, B_gdT = Buf("gmT"), Buf("gdT")
        mrow = A([512], F32)
        B_mrow = Buf("mrow")
        S.dma("sp", mrow, rows_d[:, 0:512], "mrow", writes=[B_mrow])
        wg = A([8, 8], BF16)
        B_wg = Buf("wg")
        S.dma("pool", wg, wview(w_in, C_MI, 8), "wg", writes=[B_wg])
        wv = A([8, 512], BF16)
        wo = A([8, 512], BF16)
        B_wv, B_wo = Buf("wv"), Buf("wo")
        S.dma("pool", wv, wview(w_in, C_MV, 512), "wv", writes=[B_wv])
        S.dma("pool", wo, wview(w_in, C_MO, 512), "wo", writes=[B_wo])
        wqk = [A([8, 128], BF16) for _ in range(2)]
        Bwqk = [Buf("wqk0"), Buf("wqk1")]
        qkT = A([8, T], BF16)
        B_qk = [Buf("qk%d" % i) for i in range(8)]
        pre = A([3 + T], F32)
        acc = A([T], F32)
        B_pre, B_acc = Buf("pre"), Buf("acc")
        sm = lambda: A([16, 4], F32)
        ifp = A([16, 8], F32)
        logf, tmp64, b_tm, btot, g_tm, cm, gmax, Mx, s_int, e_negm, a_col, s_old, s_loc, negM = [sm() for _ in range(14)]
        m_all = A([17, 4], F32)
        Bsmall = Buf("small")

        for t in range(NT):
            for kcx in range(8):
                S.op("pe", lambda q, t=t, kcx=kcx: q.matmul(ps[0][:, t * 8:(t + 1) * 8], lhsT=hT[:, kcx, tsl(t)], rhs=wg[:, kcx, :],
                                                           start=(kcx == 0), stop=(kcx == 7)),
                     reads=[B_hT, B_wg], writes=[Bps[0]], inc=(kcx == 7))
        S.op("dve", lambda q: q.tensor_tensor(out=ifp, in0=ps[0][:, 0:128].rearrange("p (t g) -> p t g", g=8),
                                              in1=csc("bif", 0, 8).unsqueeze(1).to_broadcast([128, 16, 8]), op=ALU.add),
             reads=[Bps[0], B_cs], writes=[Bsmall])
        ig = ifp[:, :, 0:4]
        S.op("act", lambda q: q.activation(out=tmp64, in_=ifp[:, :, 4:8], func=AF.Exp, scale=-1.0), reads=[Bsmall], writes=[Bsmall])
        S.op("act", lambda q: q.activation(out=tmp64, in_=tmp64, func=AF.Ln, bias=1.0), reads=[Bsmall], writes=[Bsmall])
        S.op("dve", lambda q: q.tensor_scalar(out=logf, in0=tmp64, scalar1=-1.0, scalar2=None, op0=ALU.mult), reads=[Bsmall], writes=[Bsmall])
        for t in range(NT):
            S.op("pe", lambda q, t=t: q.matmul(ps[1][:, t * 4:(t + 1) * 4], lhsT=tri, rhs=logf[:, t, :], start=True, stop=True),
                 reads=[Bsmall, B_kc], writes=[Bps[1]], inc=False)
            S.op("pe", lambda q, t=t: q.matmul(ps[1][:, 64 + t * 4:64 + (t + 1) * 4], lhsT=ones, rhs=logf[:, t, :], start=True, stop=True),
                 reads=[Bsmall, B_ones], writes=[Bps[1]], inc=(t == NT - 1))
        S.op("dve", lambda q: q.tensor_copy(out=b_tm, in_=ps[1][:, 0:64].rearrange("p (t g) -> p t g", g=4)), reads=[Bps[1]], writes=[Bsmall])
        S.op("dve", lambda q: q.tensor_copy(out=btot, in_=ps[1][:, 64:128].rearrange("p (t g) -> p t g", g=4)), reads=[Bps[1]], writes=[Bsmall])
        S.op("dve", lambda q: q.tensor_tensor(out=g_tm, in0=ig, in1=b_tm, op=ALU.subtract), reads=[Bsmall], writes=[Bsmall])
        rdiag = [A([4, 128], F32) for _ in range(2)]
        Brd = [Buf("rd0"), Buf("rd1")]
        Gm = [A([4, 128], F32) for _ in range(2)]
        BGm = [Buf("Gm0"), Buf("Gm1")]
        identbc = ident.unsqueeze(1).to_broadcast([128, 4, 128])
        def pass1_tile(t):
            k = t % 2
            pb = 2 + k
            S.op("dve", lambda q, k=k, t=t: q.tensor_tensor(out=rdiag[k], in0=identbc, in1=g_tm[:, t, :].unsqueeze(2).to_broadcast([128, 4, 128]), op=ALU.mult),
                 reads=[Bsmall, B_kc], writes=[Brd[k]])
            S.op("pe", lambda q, k=k, pb=pb: q.matmul(ps[pb][:], lhsT=ones, rhs=rdiag[k].rearrange("p a b -> p (a b)"), start=True, stop=True),
                 reads=[Brd[k], B_ones], writes=[Bps[pb]])
            psv = ps[pb][:].rearrange("p (a b) -> p a b", a=4)
            S.op("dve", lambda q, t=t, psv=psv: q.tensor_reduce(out=gmax[:, t, :], in_=psv, axis=AX.X, op=ALU.max), reads=[Bps[pb]], writes=[Bsmall])
            S.op("dve", lambda q, k=k, psv=psv: q.tensor_tensor(out=Gm[k], in0=psv, in1=maskL.unsqueeze(1).to_broadcast([128, 4, 128]), op=ALU.add),
                 reads=[Bps[pb], B_kc], writes=[BGm[k]])
            S.op("dve", lambda q, k=k, t=t: q.tensor_reduce(out=cm[:, t, :], in_=Gm[k], axis=AX.X, op=ALU.max), reads=[BGm[k]], writes=[Bsmall])
        pre2 = [pre, A([3 + T], F32)]
        acc2 = [acc, A([T], F32)]
        Bpre2 = [B_pre, Buf("pre1")]
        Bacc2 = [B_acc, Buf("acc1")]
        for i in range(2):
            S.op("dve", lambda q, i=i: q.memset(pre2[i][:, 0:3], 0.0), writes=[Bpre2[i]])

        def qk_chunk(ch):
            k = ch % 2
            pr, ac_, Bp, Ba = pre2[k], acc2[k], Bpre2[k], Bacc2[k]
            S.dma("pool", wqk[k], wview(w_in, ch * 128, 128), "wqk%d" % k, writes=[Bwqk[k]])
            for tt in range(NQ):
                pb = 4 + (tt % 2)
                for kcx in range(8):
                    S.op("pe", lambda q, kcx=kcx, tt=tt, pb=pb: q.matmul(ps[pb][:], lhsT=wqk[k][:, kcx, :], rhs=hT[:, kcx, tsl(tt, 512)],
                                                                       start=(kcx == 0), stop=(kcx == 7)),
                         reads=[Bwqk[k], B_hT], writes=[Bps[pb]], inc=(kcx == 7))
                S.op("act", lambda q, tt=tt, pb=pb: q.activation(out=pr[:, 3 + tt * 512:3 + (tt + 1) * 512], in_=ps[pb][:], func=AF.Copy),
                     reads=[Bps[pb]], writes=[Bp])

        def qk_conv(ch):
            k = ch % 2
            pr, ac_, Bp, Ba = pre2[k], acc2[k], Bpre2[k], Bacc2[k]
            S.op("act", lambda q: q.activation(out=ac_, in_=pr[:, 3:3 + T], func=AF.Identity, scale=csc("mcw", ch * 4 + 3), bias=csc("mcb", ch)),
                 reads=[Bp, B_cs], writes=[Ba])
            for tap in (2, 1, 0):
                S.op("dve", lambda q, tap=tap: q.scalar_tensor_tensor(out=ac_, in0=pr[:, tap:tap + T], scalar=csc("mcw", ch * 4 + tap), in1=ac_,
                                                                      op0=ALU.mult, op1=ALU.add),
                     reads=[Bp, Ba, B_cs], writes=[Ba])
            S.op("act", lambda q: q.activation(out=qkT[:, ch, :], in_=ac_, func=AF.Silu), reads=[Ba], writes=[B_qk[ch]])

        qk_chunk(0)
        for ch in range(8):
            if ch + 1 < 8:
                qk_chunk(ch + 1)
            qk_conv(ch)
            pass1_tile(2 * ch)
            pass1_tile(2 * ch + 1)
        S.op("dve", lambda q: q.memset(m_all[:, 0, :], 0.0), writes=[Bsmall])
        tmp4 = tmp64[:, 0, :]
        for t in range(NT):
            S.op("dve", lambda q, t=t: q.tensor_tensor(out=tmp4, in0=m_all[:, t, :], in1=gmax[:, t, :], op=ALU.max), reads=[Bsmall], writes=[Bsmall])
            S.op("dve", lambda q, t=t: q.tensor_tensor(out=m_all[:, t + 1, :], in0=tmp4, in1=btot[:, t, :], op=ALU.add), reads=[Bsmall], writes=[Bsmall])
        mprev = m_all[:, 0:16, :]
        mnext = m_all[:, 1:17, :]
        sop = lambda fn: S.op("dve", fn, reads=[Bsmall, B_kc], writes=[Bsmall])
        aop = lambda fn: S.op("act", fn, reads=[Bsmall, B_kc], writes=[Bsmall])
        sop(lambda q: q.tensor_tensor(out=Mx, in0=cm, in1=mprev, op=ALU.max))
        sop(lambda q: q.tensor_tensor(out=s_int, in0=mprev, in1=Mx, op=ALU.subtract))
        aop(lambda q: q.activation(out=s_int, in_=s_int, func=AF.Exp))
        sop(lambda q: q.tensor_tensor(out=e_negm, in0=b_tm, in1=Mx, op=ALU.add))
        aop(lambda q: q.activation(out=e_negm, in_=e_negm, func=AF.Exp, scale=-1.0))
        sop(lambda q: q.tensor_tensor(out=a_col, in0=g_tm, in1=gmax, op=ALU.subtract))
        aop(lambda q: q.activation(out=a_col, in_=a_col, func=AF.Exp, bias=lnkc))
        sop(lambda q: q.tensor_tensor(out=s_old, in0=btot, in1=mprev, op=ALU.add))
        sop(lambda q: q.tensor_tensor(out=s_old, in0=s_old, in1=mnext, op=ALU.subtract))
        aop(lambda q: q.activation(out=s_old, in_=s_old, func=AF.Exp))
        sop(lambda q: q.tensor_tensor(out=s_loc, in0=btot, in1=gmax, op=ALU.add))
        sop(lambda q: q.tensor_tensor(out=s_loc, in0=s_loc, in1=mnext, op=ALU.subtract))
        aop(lambda q: q.activation(out=s_loc, in_=s_loc, func=AF.Exp))
        sop(lambda q: q.tensor_scalar(out=negM, in0=Mx, scalar1=-1.0, scalar2=None, op0=ALU.mult))

        if stop_after == "B1":
            dump("sm", s_loc, [Bsmall], [128, 16, 4], F32)
            phase_end("B1")
            return nc, dump_d
        dump("qkT", qkT, B_qk, [128, 8, T], BF16)
        if stop_after == "B2":
            phase_end("B2")
            return nc, dump_d

        xm = [A([4, 128], F32) for _ in range(2)]
        Bxm = [Buf("xm0"), Buf("xm1")]
        dmT = [A([4, 128], F32) for _ in range(2)]
        BdmT = [Buf("dmT0"), Buf("dmT1")]
        gso = [A([4, 128], F32) for _ in range(2)]
        Bgso = [Buf("gso0"), Buf("gso1")]
        vaug = [A([4, 160], BF16)[:, :, 0:129] for _ in range(2)]
        Bva = [Buf("va0"), Buf("va1")]
        sqk = [A([128], BF16) for _ in range(4)]
        akb = [A([128], BF16) for _ in range(4)]
        hmb = [A([128], BF16) for _ in range(4)]
        ints = [A([129], F32) for _ in range(4)]
        int2 = [A([129], F32) for _ in range(4)]
        tot = [A([129], F32) for _ in range(4)]
        junk = [A([128], F32) for _ in range(4)]
        cols = A([4, 8], F32)
        Bsqk, Bakb, Bhmb, Bints, Bint2, Btot, Bjunk = [[Buf("%s%d" % (n, i)) for i in range(4)] for n in ("sqk", "akb", "hmb", "ints", "int2", "tot", "junk")]
        Bcols = [Buf("cols0"), Buf("cols1")]
        Cn = A([4, 129], F32)
        Cnb = [A([4, 160], BF16)[:, :, 0:129] for _ in range(2)]
        BCn = [Buf("Cn%d" % h) for h in range(4)]
        BCnb = [[Buf("Cnb%d_%d" % (i, h)) for h in range(4)] for i in range(2)]
        S.op("dve", lambda q: q.memset(Cn, 0.0), writes=BCn)
        S.op("dve", lambda q: q.memset(Cnb[0], 0.0), writes=BCnb[0])
        for i in range(2):
            S.op("dve", lambda q, i=i: q.memset(vaug[i][:, :, 128:129], 1.0), writes=[Bva[i]])
        BX = lambda h: 4 + 2 * (h % 2)
        BY = lambda h: 5 + 2 * (h % 2)
        pS_ = lambda h: ps[BX(h)][:, 0:128]
        pI_ = lambda h: ps[BX(h)][:, 128:257]
        pN_ = lambda h: ps[BX(h)][:, 264:393]
        pC_ = lambda h: ps[BY(h)][:, 0:129]
        pK_ = lambda h: ps[BY(h)][:, 192:256].bitcast(BF16)
        pT_ = lambda h: ps[3][:, h * 64:(h + 1) * 64].bitcast(BF16)

        def pre_tile(t):
            k = t % 2
            S.op("dve", lambda q: q.tensor_tensor(out=rdiag[k], in0=identbc, in1=negM[:, t, :].unsqueeze(2).to_broadcast([128, 4, 128]), op=ALU.mult),
                 reads=[Bsmall, B_kc], writes=[Brd[k]])
            S.op("pe", lambda q: q.matmul(ps[0][:], lhsT=ones, rhs=rdiag[k].rearrange("p a b -> p (a b)"), start=True, stop=True),
                 reads=[Brd[k], B_ones], writes=[Bps[0]])
            S.op("dve", lambda q: q.tensor_tensor(out=xm[k], in0=ps[0][:].rearrange("p (a b) -> p a b", a=4),
                                                  in1=maskT.unsqueeze(1).to_broadcast([128, 4, 128]), op=ALU.add),
                 reads=[Bps[0], B_kc], writes=[Bxm[k]])
            S.op("dve", lambda q: q.tensor_tensor(out=xm[k], in0=xm[k], in1=g_tm[:, t, :].unsqueeze(2).to_broadcast([128, 4, 128]), op=ALU.add),
                 reads=[Bxm[k], Bsmall], writes=[Bxm[k]])
            S.op("act", lambda q: q.activation(out=dmT[k], in_=xm[k], func=AF.Exp), reads=[Bxm[k]], writes=[BdmT[k]])
            for (wt, Bw, pb) in ((wo, B_wo, 1), (wv, B_wv, 2)):
                for kcx in range(8):
                    S.op("pe", lambda q, wt=wt, kcx=kcx, pb=pb: q.matmul(ps[pb][:], lhsT=hT[:, kcx, tsl(t)], rhs=wt[:, kcx, :],
                                                                       start=(kcx == 0), stop=(kcx == 7)),
                         reads=[B_hT, Bw], writes=[Bps[pb]], inc=(kcx == 7))
            g2 = gso[k].rearrange("p a b -> p (a b)")
            S.op("act", lambda q: q.activation(out=g2, in_=ps[1][:], func=AF.Exp, scale=-1.0), reads=[Bps[1]], writes=[Bgso[k]])
            S.op("act", lambda q: q.activation(out=g2, in_=g2, func=AF.Ln, bias=1.0), reads=[Bgso[k]], writes=[Bgso[k]])
            S.op("act", lambda q: q.activation(out=g2, in_=g2, func=AF.Exp, scale=-1.0), reads=[Bgso[k]], writes=[Bgso[k]])
            S.op("dve", lambda q: q.tensor_tensor(out=g2, in0=g2, in1=mrow, op=ALU.mult), reads=[Bgso[k], B_mrow], writes=[Bgso[k]])
            S.op("dve", lambda q: q.tensor_copy(out=vaug[k][:, :, 0:128], in_=ps[2][:].rearrange("p (a b) -> p a b", a=4)),
                 reads=[Bps[2]], writes=[Bva[k]])

        HS = range(4)
        pre_tile(0)
        import os as _os
        _TM = int(_os.environ.get("DEV_T", NT))
        _CUT = int(_os.environ.get("DEV_CUT", 10 ** 9))

        def tile_body(t):
            k = t % 2
            cb = t % 2
            qt_ = lambda h: qkT[:, h, tsl(t)]
            kt_ = lambda h: qkT[:, 4 + h, tsl(t)]
            def st123a(hp):
                HS = (2 * hp, 2 * hp + 1)
                for h in HS:
                    S.op("pe", lambda q, h=h: q.matmul(pS_(h), lhsT=kt_(h), rhs=qt_(h), start=True, stop=True), reads=[B_qk[h], B_qk[4 + h]], writes=[Bps[BX(h)]])
                    S.op("pe", lambda q, h=h: q.transpose(out=pK_(h), in_=kt_(h), identity=identb), reads=[B_qk[4 + h], B_identb], writes=[Bps[BY(h)]])
                    S.op("pe", lambda q, h=h: q.matmul(pI_(h), lhsT=qt_(h), rhs=Cnb[cb][:, h, :], start=True, stop=True), reads=[B_qk[h], BCnb[cb][h]], writes=[Bps[BX(h)]])
                for h in HS:
                    S.op("dve", lambda q, h=h: q.tensor_tensor(out=sqk[h], in0=pS_(h), in1=dmT[k][:, h, :], op=ALU.mult), reads=[Bps[BX(h)], BdmT[k]], writes=[Bsqk[h]])
                for h in HS:
                    S.op("act", lambda q, h=h: q.activation(out=akb[h], in_=pK_(h), func=AF.Copy, scale=a_col[:, t, h:h + 1]), reads=[Bps[BY(h)], Bsmall], writes=[Bakb[h]])
                for h in HS:
                    S.op("act", lambda q, h=h: q.activation(out=ints[h], in_=pI_(h), func=AF.Copy, scale=s_int[:, t, h:h + 1]), reads=[Bps[BX(h)], Bsmall], writes=[Bints[h]])
                for h in HS:
                    S.op("pe", lambda q, h=h: q.matmul(pN_(h), lhsT=sqk[h], rhs=vaug[k][:, h, :], start=True, stop=True), reads=[Bsqk[h], Bva[k]], writes=[Bps[BX(h)]])
                for h in HS:
                    S.op("pe", lambda q, h=h: q.matmul(pC_(h), lhsT=akb[h], rhs=vaug[k][:, h, :], start=True, stop=True), reads=[Bakb[h], Bva[k]], writes=[Bps[BY(h)]])
                for h in HS:
                    S.op("dve", lambda q, h=h: q.tensor_tensor(out=tot[h], in0=pN_(h), in1=ints[h], op=ALU.add), reads=[Bps[BX(h)], Bints[h]], writes=[Btot[h]])
                for h in HS:
                    S.op("act", lambda q, h=h: q.activation(out=int2[h], in_=pC_(h), func=AF.Copy, scale=s_loc[:, t, h:h + 1]), reads=[Bps[BY(h)], Bsmall], writes=[Bint2[h]])

            def st4b5(hp):
                HS = (2 * hp, 2 * hp + 1)
                cg = cols[:, 2 * hp:2 * hp + 2, :]
                for h in HS:
                    S.op("dve", lambda q, h=h: q.scalar_tensor_tensor(out=Cn[:, h, :], in0=Cn[:, h, :], scalar=s_old[:, t, h:h + 1], in1=int2[h], op0=ALU.mult, op1=ALU.add),
                         reads=[Bint2[h], BCn[h], Bsmall], writes=[BCn[h]])
                for h in HS:
                    S.op("act", lambda q, h=h: q.activation(out=Cnb[1 - cb][:, h, :], in_=Cn[:, h, :], func=AF.Copy), reads=[BCn[h]], writes=[BCnb[1 - cb][h]])
                for h in HS:
                    S.op("act", lambda q, h=h: q.activation(out=cols[:, h, 0:1], in_=tot[h][:, 128:129], func=AF.Abs), reads=[Btot[h]], writes=[Bcols[hp]])
                S.op("dve", lambda q: q.tensor_tensor(out=cg[:, :, 1:2], in0=cg[:, :, 0:1], in1=e_negm[:, t, 2 * hp:2 * hp + 2].unsqueeze(2), op=ALU.max), reads=[Bcols[hp], Bsmall], writes=[Bcols[hp]])
                S.op("dve", lambda q: q.reciprocal(out=cg[:, :, 2:3], in_=cg[:, :, 1:2]), reads=[Bcols[hp]], writes=[Bcols[hp]])
                S.op("dve", lambda q: q.memset(cg[:, :, 3:4], 0.0), writes=[Bcols[hp]])
                for h in HS:
                    S.op("act", lambda q, h=h: q.activation(out=junk[h], in_=tot[h][:, 0:128], func=AF.Square, scale=cols[:, h, 2:3], accum_out=cols[:, h, 3:4]),
                         reads=[Btot[h], Bcols[hp]], writes=[Bjunk[h], Bcols[hp]])
                S.op("act", lambda q: q.activation(out=cg[:, :, 4:5], in_=cg[:, :, 3:4], func=AF.Ln, scale=1.0 / 128, bias=epsc), reads=[Bcols[hp], B_kc], writes=[Bcols[hp]])
                S.op("act", lambda q: q.activation(out=cg[:, :, 5:6], in_=cg[:, :, 4:5], func=AF.Exp, scale=-0.5), reads=[Bcols[hp]], writes=[Bcols[hp]])
                S.op("dve", lambda q: q.tensor_tensor(out=cg[:, :, 6:7], in0=cg[:, :, 5:6], in1=cg[:, :, 2:3], op=ALU.mult), reads=[Bcols[hp]], writes=[Bcols[hp]])
                for h in HS:
                    S.op("dve", lambda q, h=h: q.scalar_tensor_tensor(out=hmb[h], in0=tot[h][:, 0:128], scalar=cols[:, h, 6:7], in1=gso[k][:, h, :], op0=ALU.mult, op1=ALU.mult),
                         reads=[Btot[h], Bcols[hp], Bgso[k]], writes=[Bhmb[h]])
                for h in HS:
                    S.op("pe", lambda q, h=h: q.transpose(out=pT_(h), in_=hmb[h], identity=identb), reads=[Bhmb[h], B_identb], writes=[Bps[3]])
                for h in HS:
                    if h % 2 == 0:
                        S.op("act", lambda q, h=h: q.activation(out=gmT[:, h, tsl(t)], in_=pT_(h), func=AF.Copy), reads=[Bps[3]], writes=[B_gmT])
                    else:
                        S.op("dve", lambda q, h=h: q.tensor_copy(out=gmT[:, h, tsl(t)], in_=pT_(h)), reads=[Bps[3]], writes=[B_gmT])

            st123a(0)
            st123a(1)
            if t + 1 < _TM:
                pre_tile(t + 1)
            st4b5(0)
            st4b5(1)
        for t in range(_TM):
            tile_body(t)
        S.cut = 10 ** 12
        dump("gmT", gmT, [B_gmT], [128, 4, T], BF16)
        if phase_end("B"):
            return nc, dump_d

        A.reset([RX, RZ])
        Tm = A([4, 640], F32)
        B_Tm = Buf("Tm")
        S.dma("sp", Tm, biasT_d.rearrange("p (h c) -> p h c", h=4), "Tm", writes=[B_Tm])
        S.op("dve", lambda q: q.tensor_tensor(out=Tm, in0=Tm, in1=maskA.unsqueeze(1).to_broadcast([128, 4, 640]), op=ALU.add),
             reads=[B_Tm, B_kc], writes=[B_Tm])
        drow = A([512], F32)
        B_drow = Buf("drow")
        S.dma("sp", drow, rows_d[:, 512:1024], "drow", writes=[B_drow])
        S.op("dve", lambda q: q.tensor_scalar(out=drow, in0=drow, scalar1=1.0 - LAM_INIT, scalar2=None, op0=ALU.mult), reads=[B_drow], writes=[B_drow])
        lamc = A([8], F32)
        lprod = A([2, 64], F32)
        B_lam = Buf("lam")
        dalv = csc("dal", 0, 256).rearrange("p (a b d) -> p a b d", a=2, b=2)
        S.op("dve", lambda q: q.tensor_tensor(out=lprod, in0=dalv[:, :, 0, :], in1=dalv[:, :, 1, :], op=ALU.mult), reads=[B_cs], writes=[B_lam])
        S.op("dve", lambda q: q.tensor_reduce(out=lamc[:, 0:2], in_=lprod, axis=AX.X, op=ALU.add), reads=[B_lam], writes=[B_lam])
        S.op("act", lambda q: q.activation(out=lamc[:, 0:2], in_=lamc[:, 0:2], func=AF.Exp), reads=[B_lam], writes=[B_lam])
        S.op("dve", lambda q: q.tensor_tensor(out=lamc[:, 2:3], in0=lamc[:, 1:2], in1=lamc[:, 0:1], op=ALU.subtract), reads=[B_lam], writes=[B_lam])
        S.op("dve", lambda q: q.tensor_scalar(out=lamc[:, 3:4], in0=lamc[:, 2:3], scalar1=-LAM_INIT, scalar2=None, op0=ALU.add), reads=[B_lam], writes=[B_lam])
        neglam = lamc[:, 3:4]

        wd = [A([8, 3, 128], BF16) for _ in range(2)]
        Bwd = [Buf("wd0"), Buf("wd1")]
        dqT = [A([T], BF16) for _ in range(2)]
        dkT = [A([T], BF16) for _ in range(2)]
        dva = [A([16, 160], BF16)[:, :, 0:129] for _ in range(2)]
        Bdq, Bdk, Bdva = [[Buf("%s%d" % (n, i)) for i in range(2)] for n in ("dq", "dk", "dva")]
        for i in range(2):
            S.op("dve", lambda q, i=i: q.memset(dva[i][:, :, 128:129], 1.0), writes=[Bdva[i]])
        xsb = [A([512], F32) for _ in range(2)]
        Bxsb = [Buf("xsb0"), Buf("xsb1")]
        PT = [[A([512], BF16) for _ in range(2)] for _ in range(2)]
        BPT = [[Buf("PT%d_%d" % (i, c)) for c in range(2)] for i in range(2)]
        t1 = [A([128], F32) for _ in range(8)]
        dd = [A([128], F32) for _ in range(8)]
        ob = [A([128], BF16) for _ in range(8)]
        jk2 = [A([128], F32) for _ in range(8)]
        cl2 = [A([8], F32) for _ in range(8)]
        Bt1, Bdd, Bob, Bjk2, Bcl2 = [[Buf("%s%d" % (n, i)) for i in range(8)] for n in ("t1", "dd", "ob", "jk2", "cl2")]
        Oreg = []
        BO = []
        for c in range(2):
            row, brow = [], []
            for i in range(4):
                idx = c * 4 + i
                bank, off = 4 + idx // 3, (idx % 3) * 160
                row.append(ps[bank][:, off:off + 129])
                brow.append(Bps[bank])
            Oreg.append(row)
            BO.append(brow)
        B_misc = Bps[7]
        pT7 = ps[7][:, 448:512].bitcast(BF16)
        B_pT7 = Bps[7]
        fin = [0]

        def proj_chunks(h):
            sl = h % 2
            chunks = []

            def load():
                for j_, c0 in enumerate((C_DQ, C_DK, C_DV)):
                    S.dma("pool", wd[sl][:, :, j_, :], wview(w_in, c0 + h * 128, 128), "wd%d" % sl, writes=[Bwd[sl]])
            for (j_, dst, Bd, scl) in ((0, dqT[sl], Bdq[sl], 0.125), (1, dkT[sl], Bdk[sl], 1.0)):
                for tt in range(NQ):
                    def ch(j_=j_, dst=dst, Bd=Bd, scl=scl, tt=tt):
                        for kcx in range(8):
                            S.op("pe", lambda q, kcx=kcx: q.matmul(ps[7][:, 0:512], lhsT=wd[sl][:, kcx, j_, :], rhs=hT[:, kcx, tsl(tt, 512)],
                                                                  start=(kcx == 0), stop=(kcx == 7)),
                                 reads=[Bwd[sl], B_hT], writes=[Bps[7]], inc=(kcx == 7))
                        S.op("dve", lambda q: q.tensor_scalar(out=dst[:, tsl(tt, 512)], in0=ps[7][:, 0:512], scalar1=scl, scalar2=None, op0=ALU.mult),
                             reads=[Bps[7]], writes=[Bd])
                    chunks.append(ch)
            for g in range(4):
                def ch(g=g):
                    for tq in range(4):
                        t = g * 4 + tq
                        for kcx in range(8):
                            S.op("pe", lambda q, kcx=kcx, t=t, tq=tq: q.matmul(ps[7][:, tq * 128:(tq + 1) * 128], lhsT=hT[:, kcx, tsl(t)], rhs=wd[sl][:, kcx, 2, :],
                                                                            start=(kcx == 0), stop=(kcx == 7)),
                                 reads=[Bwd[sl], B_hT], writes=[Bps[7]], inc=(kcx == 7 and tq == 3))
                    S.op("dve", lambda q: q.tensor_copy(out=dva[sl][:, g * 4:(g + 1) * 4, 0:128], in_=ps[7][:, 0:512].rearrange("p (a b) -> p a b", a=4)),
                         reads=[Bps[7]], writes=[Bdva[sl]])
                chunks.append(ch)
            return load, chunks

        def it_info(h, qt, kb, n):
            imin = max(0, kb - 4 * qt)
            return dict(h=h, sl=h % 2, qt=qt, kb=kb, k0=kb * 128, imin=imin, qs=qt * 512 + imin * 128, W=512 - imin * 128,
                        near=(kb >= 4 * qt - 1), slot=n % 2, n=n)

        def emit_S(I):
            for c in range(2):
                pS = ps[I["slot"] * 2 + c][:, 0:I["W"]]
                S.op("pe", lambda q, pS=pS, c=c, I=I: q.matmul(pS, lhsT=dkT[I["sl"]][c * 64:(c + 1) * 64, I["k0"]:I["k0"] + 128],
                                                              rhs=dqT[I["sl"]][c * 64:(c + 1) * 64, I["qs"]:I["qs"] + I["W"]], start=True, stop=True),
                     reads=[Bdk[I["sl"]], Bdq[I["sl"]]], writes=[Bps[I["slot"] * 2 + c]])

        def emit_exp(I):
            W, h, slot = I["W"], I["h"], I["slot"]
            for c in range(2):
                pS = ps[slot * 2 + c][:, 0:W]
                bS = Bps[slot * 2 + c]
                pt = PT[slot][c][:, 0:W]
                if I["near"]:
                    cst = I["qs"] - I["k0"]
                    xk = c
                    S.op("dve", lambda q, xk=xk, pS=pS, cst=cst: q.tensor_tensor(out=xsb[xk][:, 0:W], in0=pS, in1=Tm[:, h, cst:cst + W], op=ALU.add),
                         reads=[bS, B_Tm], writes=[Bxsb[xk]])
                    S.op("act", lambda q, pt=pt, xk=xk: q.activation(out=pt, in_=xsb[xk][:, 0:W], func=AF.Exp), reads=[Bxsb[xk]], writes=[BPT[slot][c]])
                else:
                    S.op("act", lambda q, pt=pt, pS=pS: q.activation(out=pt, in_=pS, func=AF.Exp, bias=csc("c31", h)),
                         reads=[bS, B_cs], writes=[BPT[slot][c]])

        def emit_PV(I):
            slot, imin, sl, kb, qt = I["slot"], I["imin"], I["sl"], I["kb"], I["qt"]
            for c in range(2):
                for i in range(imin, 4):
                    S.op("pe", lambda q, c=c, i=i: q.matmul(
                        Oreg[c][i], lhsT=PT[slot][c][:, (i - imin) * 128:(i - imin + 1) * 128], rhs=dva[sl][:, kb, :],
                        start=(kb == 0 and (c * 4 + i) % 3 == 0), stop=(kb == 4 * qt + i), skip_group_check=True),
                        reads=[BPT[slot][c], Bdva[sl]], writes=[BO[c][i]], inc=(kb == 4 * qt + i or i == 3))

        def finalize(h, qt):
            later = []
            fs = []
            for i in range(4):
                f = fin[0] % 8
                fin[0] += 1
                fs.append(f)
                c_ = cl2[f]
                S.op("dve", lambda q, c_=c_, i=i: q.reciprocal(out=c_[:, 0:1], in_=Oreg[0][i][:, 128:129]), reads=[BO[0][i]], writes=[Bcl2[f]])
                S.op("dve", lambda q, c_=c_, i=i: q.reciprocal(out=c_[:, 1:2], in_=Oreg[1][i][:, 128:129]), reads=[BO[1][i]], writes=[Bcl2[f]])
                S.op("dve", lambda q, c_=c_: q.tensor_tensor(out=c_[:, 2:3], in0=c_[:, 1:2], in1=neglam, op=ALU.mult), reads=[Bcl2[f], B_lam], writes=[Bcl2[f]])
            for i in range(4):
                f = fs[i]
                c_ = cl2[f]
                S.op("act", lambda q, c_=c_, i=i, f=f: q.activation(out=t1[f], in_=Oreg[0][i][:, 0:128], func=AF.Copy, scale=c_[:, 0:1]),
                     reads=[BO[0][i], Bcl2[f]], writes=[Bt1[f]])
                S.op("act", lambda q, c_=c_, i=i, f=f: q.activation(out=dd[f], in_=Oreg[1][i][:, 0:128], func=AF.Copy, scale=c_[:, 2:3]),
                     reads=[BO[1][i], Bcl2[f]], writes=[Bdd[f]])

            def stage2(i, f):
                c_ = cl2[f]
                qb = qt * 4 + i
                S.op("dve", lambda q: q.tensor_tensor(out=dd[f], in0=dd[f], in1=t1[f], op=ALU.add), reads=[Bdd[f], Bt1[f]], writes=[Bdd[f]])
                S.op("dve", lambda q: q.memset(c_[:, 3:4], 0.0), writes=[Bcl2[f]])
                S.op("act", lambda q: q.activation(out=jk2[f], in_=dd[f], func=AF.Square, accum_out=c_[:, 3:4]), reads=[Bdd[f]], writes=[Bjk2[f], Bcl2[f]])
                S.op("act", lambda q: q.activation(out=c_[:, 4:5], in_=c_[:, 3:4], func=AF.Ln, scale=1.0 / 128, bias=epsc), reads=[Bcl2[f], B_kc], writes=[Bcl2[f]])
                S.op("act", lambda q: q.activation(out=c_[:, 5:6], in_=c_[:, 4:5], func=AF.Exp, scale=-0.5), reads=[Bcl2[f]], writes=[Bcl2[f]])
                S.op("dve", lambda q: q.scalar_tensor_tensor(out=ob[f], in0=dd[f], scalar=c_[:, 5:6], in1=drow[:, h * 128:(h + 1) * 128], op0=ALU.mult, op1=ALU.mult),
                     reads=[Bdd[f], Bcl2[f], B_drow], writes=[Bob[f]])

            def stage3(i, f):
                qb = qt * 4 + i
                S.op("pe", lambda q: q.transpose(out=pT7, in_=ob[f], identity=identb), reads=[Bob[f], B_identb], writes=[Bps[7]])
                S.op("dve", lambda q: q.tensor_copy(out=gdT[:, h, tsl(qb)], in_=pT7), reads=[Bps[7]], writes=[B_gdT])
            for i in range(4):
                later.append(lambda i=i, f=fs[i]: stage2(i, f))
            for i in range(4):
                later.append(lambda i=i, f=fs[i]: stage3(i, f))
            return later

        iters = []
        for h in range(4):
            for qt in range(NQ):
                for kb in range(4 * qt + 4):
                    iters.append(it_info(h, qt, kb, len(iters)))
        ld0, ch0 = proj_chunks(0)
        ld0()
        for ch in ch0:
            ch()
        pending = []
        fin_later = []
        emit_S(iters[0])
        for n, I in enumerate(iters):
            if I["qt"] == 0 and I["kb"] == 0 and I["h"] + 1 < 4:
                ld, pending = proj_chunks(I["h"] + 1)
                ld()
            nxt = iters[n + 1] if n + 1 < len(iters) else None
            if nxt is not None and nxt["h"] != I["h"]:
                for ch in pending:
                    ch()
                pending = []
            if nxt is not None:
                emit_S(nxt)
            emit_exp(I)
            emit_PV(I)
            if pending:
                pending.pop(0)()
            for _ in range(2):
                if fin_later:
                    fin_later.pop(0)()
            if I["kb"] == 4 * I["qt"] + 3:
                fin_later.extend(finalize(I["h"], I["qt"]))
        while fin_later:
            fin_later.pop(0)()
        dump("gdT", gdT, [B_gdT], [128, 4, T], BF16)
        if phase_end("C"):
            return nc, dump_d

        A.reset([RZ])
        RX.reset()
        xT = RX.alloc([8, T], F32)
        B_xT = [Buf("xT%d" % c) for c in range(8)]
        for c in range(8):
            S.dma("sp", xT[:, c, :], xT_d[c * 128:(c + 1) * 128, :], "xT%d" % c, writes=[B_xT[c]])
        mixT = A([8, T], BF16)
        B_mix = [Buf("mix%d" % j) for j in range(8)]
        wmg = [A([24, 128], BF16) for _ in range(2)]
        Bwmg = [Buf("wmg0"), Buf("wmg1")]
        sa = [A([512], F32) for _ in range(2)]
        sb_ = [A([512], F32) for _ in range(2)]
        Bsa, Bsb = [Buf("sa0"), Buf("sa1")], [Buf("sb0"), Buf("sb1")]
        it = 0
        for j in range(8):
            k = j % 2
            S.dma("pool", wmg[k][:, 0:8, :], wview(w_in, C_GA + j * 128, 128), "wmg%d" % k, writes=[Bwmg[k]])
            S.dma("pool", wmg[k][:, 8:16, :], wview(w_in, C_GB + j * 128, 128), "wmg%d" % k, writes=[Bwmg[k]])
            S.dma("pool", wmg[k][:, 16:20, :], wview(w_br_m, j * 128, 128), "wmg%d" % k, writes=[Bwmg[k]])
            S.dma("pool", wmg[k][:, 20:24, :], wview(w_br_d, j * 128, 128), "wmg%d" % k, writes=[Bwmg[k]])
            for tt in range(NQ):
                s = it % 2
                it += 1
                pb = [4 * s + i for i in range(4)]
                for kcx in range(8):
                    S.op("pe", lambda q, k=k, kcx=kcx, tt=tt, b=pb[0]: q.matmul(ps[b][:], lhsT=wmg[k][:, kcx, :], rhs=hT[:, kcx, tsl(tt, 512)], start=(kcx == 0), stop=(kcx == 7)),
                         reads=[Bwmg[k], B_hT], writes=[Bps[pb[0]]], inc=(kcx == 7))
                for kcx in range(8):
                    S.op("pe", lambda q, k=k, kcx=kcx, tt=tt, b=pb[1]: q.matmul(ps[b][:], lhsT=wmg[k][:, 8 + kcx, :], rhs=hT[:, kcx, tsl(tt, 512)], start=(kcx == 0), stop=(kcx == 7)),
                         reads=[Bwmg[k], B_hT], writes=[Bps[pb[1]]], inc=(kcx == 7))
                for kcx in range(4):
                    S.op("pe", lambda q, k=k, kcx=kcx, tt=tt, b=pb[2]: q.matmul(ps[b][:], lhsT=wmg[k][:, 16 + kcx, :], rhs=gmT[:, kcx, tsl(tt, 512)], start=(kcx == 0), stop=(kcx == 3)),
                         reads=[Bwmg[k], B_gmT], writes=[Bps[pb[2]]], inc=(kcx == 3))
                for kcx in range(4):
                    S.op("pe", lambda q, k=k, kcx=kcx, tt=tt, b=pb[3]: q.matmul(ps[b][:], lhsT=wmg[k][:, 20 + kcx, :], rhs=gdT[:, kcx, tsl(tt, 512)], start=(kcx == 0), stop=(kcx == 3)),
                         reads=[Bwmg[k], B_gdT], writes=[Bps[pb[3]]], inc=(kcx == 3))
                S.op("act", lambda q, s=s, b=pb[0]: q.activation(out=sa[s], in_=ps[b][:], func=AF.Sigmoid), reads=[Bps[pb[0]]], writes=[Bsa[s]])
                S.op("act", lambda q, s=s, b=pb[1]: q.activation(out=sb_[s], in_=ps[b][:], func=AF.Sigmoid), reads=[Bps[pb[1]]], writes=[Bsb[s]])
                S.op("dve", lambda q, s=s, b=pb[2]: q.tensor_tensor(out=sa[s], in0=ps[b][:], in1=sa[s], op=ALU.mult), reads=[Bps[pb[2]], Bsa[s]], writes=[Bsa[s]])
                S.op("dve", lambda q, s=s, b=pb[3]: q.tensor_tensor(out=sb_[s], in0=ps[b][:], in1=sb_[s], op=ALU.mult), reads=[Bps[pb[3]], Bsb[s]], writes=[Bsb[s]])
                S.op("dve", lambda q, s=s, j=j, tt=tt: q.tensor_tensor(out=mixT[:, j, tsl(tt, 512)], in0=sa[s], in1=sb_[s], op=ALU.add),
                     reads=[Bsa[s], Bsb[s]], writes=[B_mix[j]])
        wob = [A([8, 128], BF16) for _ in range(2)]
        Bwob = [Buf("wob0"), Buf("wob1")]
        it = 0
        for c in range(8):
            k = c % 2
            S.dma("pool", wob[k], wview(w_out, c * 128, 128), "wob%d" % k, writes=[Bwob[k]])
            for tt in range(NQ):
                pb = it % 8
                it += 1
                for j in range(8):
                    S.op("pe", lambda q, k=k, j=j, tt=tt, pb=pb: q.matmul(ps[pb][:], lhsT=wob[k][:, j, :], rhs=mixT[:, j, tsl(tt, 512)], start=(j == 0), stop=(j == 7)),
                         reads=[Bwob[k], B_mix[j]], writes=[Bps[pb]], inc=(j == 7))
                S.op("dve", lambda q, c=c, tt=tt, pb=pb: q.tensor_tensor(out=xT[:, c, tsl(tt, 512)], in0=ps[pb][:], in1=xT[:, c, tsl(tt, 512)], op=ALU.add),
                     reads=[Bps[pb], B_xT[c]], writes=[B_xT[c]])
        dump("x1", xT, B_xT, [128, 8, T], F32)
        if phase_end("D"):
            return nc, dump_d

        A.reset([RY, RZ])

        def srcX(tt, c):
            return xT[:, c, tsl(tt, 512)], [B_xT[c]]

        rmsnorm(srcX, "g_ffn", dstA, A, psbank=0)
        S.barrier()
        A.reset([RY, RZ])
        ub = [[A([2 + 1024], F32) for _ in range(2)] for _ in range(2)]
        Bub = [[Buf("ub%d_%d" % (i, v)) for v in range(2)] for i in range(2)]
        wdn = [A([NJ, 128], BF16) for _ in range(2)]
        Bwdn = [Buf("wdn0"), Buf("wdn1")]
        halo = A([44, 2], F32)
        B_halo = Buf("halo")
        actT = A([NJ, 1024], BF16)
        B_act = [Buf("act%d" % j) for j in range(NJ)]
        ac = [[A([1024], F32) for _ in range(2)] for _ in range(2)]
        Bac = [[Buf("ac%d_%d" % (i, v)) for v in range(2)] for i in range(2)]
        wu = [A([2, 8, 128], BF16) for _ in range(2)]
        Bwu = [Buf("wu0"), Buf("wu1")]
        itj = 0
        itd = 0
        for half in range(2):
            for j in range(NJ):
                k = itj % 2
                itj += 1
                S.dma("pool", wu[k][:, 0, :, :], wview(w_up, j * 128, 128), "wu%d" % k, writes=[Bwu[k]])
                S.dma("pool", wu[k][:, 1, :, :], wview(w_up, DFF + j * 128, 128), "wu%d" % k, writes=[Bwu[k]])
                for vg in range(2):
                    jj = vg * NJ + j
                    for tl in range(2):
                        tt = half * 2 + tl
                        pb = 4 * k + vg * 2 + tl
                        for kcx in range(8):
                            S.op("pe", lambda q, k=k, vg=vg, kcx=kcx, tt=tt, pb=pb: q.matmul(ps[pb][:], lhsT=wu[k][:, vg, kcx, :], rhs=hT[:, kcx, tsl(tt, 512)],
                                                                                          start=(kcx == 0), stop=(kcx == 7)),
                                 reads=[Bwu[k], B_hT], writes=[Bps[pb]], inc=(kcx == 7))
                        S.op("act", lambda q, vg=vg, tl=tl, pb=pb, k=k: q.activation(out=ub[k][vg][:, 2 + tl * 512:2 + (tl + 1) * 512], in_=ps[pb][:], func=AF.Copy),
                             reads=[Bps[pb]], writes=[Bub[k][vg]])
                    if half == 0:
                        S.op("dve", lambda q, vg=vg, k=k: q.memset(ub[k][vg][:, 0:2], 0.0), writes=[Bub[k][vg]])
                        S.op("dve", lambda q, vg=vg, jj=jj, k=k: q.tensor_copy(out=halo[:, jj, :], in_=ub[k][vg][:, 1024:1026]), reads=[Bub[k][vg]], writes=[B_halo])
                    else:
                        S.op("dve", lambda q, vg=vg, jj=jj, k=k: q.tensor_copy(out=ub[k][vg][:, 0:2], in_=halo[:, jj, :]), reads=[B_halo], writes=[Bub[k][vg]])
                    S.op("act", lambda q, vg=vg, jj=jj, k=k: q.activation(out=ac[k][vg], in_=ub[k][vg][:, 2:1026], func=AF.Identity, scale=csc("fcw", jj * 3 + 2), bias=csc("fcb", jj)),
                         reads=[Bub[k][vg], B_cs], writes=[Bac[k][vg]])
                    for tap in (1, 0):
                        S.op("dve", lambda q, vg=vg, jj=jj, tap=tap, k=k: q.scalar_tensor_tensor(out=ac[k][vg], in0=ub[k][vg][:, tap:tap + 1024], scalar=csc("fcw", jj * 3 + tap), in1=ac[k][vg],
                                                                                           op0=ALU.mult, op1=ALU.add),
                             reads=[Bub[k][vg], Bac[k][vg], B_cs], writes=[Bac[k][vg]])
                S.op("act", lambda q, k=k: q.activation(out=ac[k][1], in_=ac[k][1], func=AF.Gelu_apprx_tanh), reads=[Bac[k][1]], writes=[Bac[k][1]])
                S.op("dve", lambda q, j=j, k=k: q.tensor_tensor(out=actT[:, j, :], in0=ac[k][1], in1=ac[k][0], op=ALU.mult), reads=[Bac[k][1], Bac[k][0]], writes=[B_act[j]])
            for c in range(8):
                k = itd % 2
                S.dma("pool", wdn[k], w_down[:, c * 128:(c + 1) * 128].rearrange("(j p) n -> p j n", p=128), "wdn%d" % k, writes=[Bwdn[k]])
                for tl in range(2):
                    tt = half * 2 + tl
                    pb = (itd * 2 + tl) % 8
                    for j in range(NJ):
                        S.op("pe", lambda q, k=k, j=j, tl=tl, pb=pb: q.matmul(ps[pb][:], lhsT=wdn[k][:, j, :], rhs=actT[:, j, tsl(tl, 512)], start=(j == 0), stop=(j == NJ - 1)),
                             reads=[Bwdn[k], B_act[j]], writes=[Bps[pb]], inc=(j == NJ - 1))
                    S.op("dve", lambda q, c=c, tt=tt, pb=pb: q.tensor_tensor(out=xT[:, c, tsl(tt, 512)], in0=ps[pb][:], in1=xT[:, c, tsl(tt, 512)], op=ALU.add),
                         reads=[Bps[pb], B_xT[c]], writes=[B_xT[c]])
                itd += 1
        dump("x2", xT, B_xT, [128, 8, T], F32)
        if phase_end("E"):
            return nc, dump_d

        A.reset([RY, RZ])
        rmsnorm(srcX, "g_ple", dstA, A, psbank=0)
        pTb = A([2, T], BF16)
        B_pTb = Buf("pTb")
        S.dma("pool", pTb, pT_d.rearrange("(kc p) t -> p kc t", p=128), "pTb", writes=[B_pTb])
        wpg = [A([10, 128], BF16) for _ in range(2)]
        Bwpg = [Buf("wpg0"), Buf("wpg1")]
        sg = [A([512], F32) for _ in range(2)]
        Bsg = [Buf("sg0"), Buf("sg1")]
        it = 0
        for c in range(8):
            k = c % 2
            S.dma("pool", wpg[k][:, 0:8, :], wview(w_pg, c * 128, 128), "wpg%d" % k, writes=[Bwpg[k]])
            S.dma("pool", wpg[k][:, 8:10, :], wview(w_pl, c * 128, 128), "wpg%d" % k, writes=[Bwpg[k]])
            for tt in range(NQ):
                s = it % 2
                it += 1
                pg, pe_ = 2 + 2 * s, 3 + 2 * s
                for kcx in range(8):
                    S.op("pe", lambda q, k=k, kcx=kcx, tt=tt, pg=pg: q.matmul(ps[pg][:], lhsT=wpg[k][:, kcx, :], rhs=hT[:, kcx, tsl(tt, 512)], start=(kcx == 0), stop=(kcx == 7)),
                         reads=[Bwpg[k], B_hT], writes=[Bps[pg]], inc=(kcx == 7))
                for kcx in range(2):
                    S.op("pe", lambda q, k=k, kcx=kcx, tt=tt, pe_=pe_: q.matmul(ps[pe_][:], lhsT=wpg[k][:, 8 + kcx, :], rhs=pTb[:, kcx, tsl(tt, 512)], start=(kcx == 0), stop=(kcx == 1)),
                         reads=[Bwpg[k], B_pTb], writes=[Bps[pe_]], inc=(kcx == 1))
                S.op("act", lambda q, s=s, pg=pg: q.activation(out=sg[s], in_=ps[pg][:], func=AF.Sigmoid), reads=[Bps[pg]], writes=[Bsg[s]])
                S.op("dve", lambda q, s=s, pe_=pe_: q.tensor_tensor(out=sg[s], in0=ps[pe_][:], in1=sg[s], op=ALU.mult), reads=[Bps[pe_], Bsg[s]], writes=[Bsg[s]])
                S.op("dve", lambda q, s=s, c=c, tt=tt: q.tensor_tensor(out=xT[:, c, tsl(tt, 512)], in0=sg[s], in1=xT[:, c, tsl(tt, 512)], op=ALU.add),
                     reads=[Bsg[s], B_xT[c]], writes=[B_xT[c]])
        if phase_end("F"):
            return nc, dump_d

        A.reset([RY, RZ])
        ost = [A([512], F32) for _ in range(4)]
        Bost = [Buf("ost%d" % i) for i in range(4)]
        cnt = [0]

        def dstG(tt, c):
            i = cnt[0] % 4
            cnt[0] += 1

            def after(i=i, tt=tt, c=c):
                S.dma("sp", outT_d[c * 128:(c + 1) * 128, tsl(tt, 512)], ost[i], "ost%d" % i, reads=[Bost[i]])
            return ost[i], [Bost[i]], after

        rmsnorm(srcX, "g_fin", dstG, A, psbank=0)
        phase_end("G")
    return nc, dump_d


def _t5_bucket(n):
    n = np.maximum(n, 0)
    max_exact = 16
    nf = np.maximum(n, 1).astype(np.float32)
    large = max_exact + (np.log(nf / np.float32(max_exact)) / np.float32(math.log(128 / max_exact))
                         * np.float32(32 - max_exact)).astype(np.int32)
    large = np.minimum(large, 31)
    return np.where(n < max_exact, n, large)


def _prep_shared(inp):
    f = np.float32
    cs = np.zeros((128, CS_N), f)
    cm = lambda v, n: np.ascontiguousarray(np.asarray(v, f).reshape(n, 128).T)
    cs[:, CS["g_mix"]:CS["g_mix"] + 8] = cm(inp["norm_mix_g"][0], 8)
    cs[:, CS["g_ffn"]:CS["g_ffn"] + 8] = cm(inp["norm_ffn_g"][0], 8)
    cs[:, CS["g_ple"]:CS["g_ple"] + 8] = cm(inp["norm_ple_g"][0], 8)
    cs[:, CS["g_fin"]:CS["g_fin"] + 8] = cm(inp["norm_final_g"], 8)
    cs[:, CS["mcw"]:CS["mcw"] + 32] = np.asarray(inp["m_conv_w"][0], f).reshape(4, 8, 128).transpose(2, 1, 0).reshape(128, 32)
    cs[:, CS["mcb"]:CS["mcb"] + 8] = cm(inp["m_conv_b"][0], 8)
    cs[:, CS["fcw"]:CS["fcw"] + 132] = np.asarray(inp["ffn_conv_w"][0], f).reshape(3, 44, 128).transpose(2, 1, 0).reshape(128, 132)
    cs[:, CS["fcb"]:CS["fcb"] + 44] = cm(inp["ffn_conv_b"][0], 44)
    cs[:, CS["bif"]:CS["bif"] + 8] = np.broadcast_to(np.asarray(inp["b_if"][0], f), (128, 8))
    rb = np.asarray(inp["rel_bias"], f)
    cs[:, CS["c31"]:CS["c31"] + 4] = np.broadcast_to(rb[31], (128, 4))
    cs[:, CS["dal"]:CS["dal"] + 256] = np.broadcast_to(np.asarray(inp["da_lambda"][0], f).reshape(256), (128, 256))
    rows = np.zeros((128, 1024), f)
    rows[:, 0:512] = np.broadcast_to(np.asarray(inp["m_norm_g"][0], f), (128, 512))
    rows[:, 512:1024] = np.broadcast_to(np.asarray(inp["da_norm_g"][0], f), (128, 512))
    kl = np.arange(128)[:, None]
    cc = np.arange(640)[None, :]
    bucket = _t5_bucket(cc - kl)
    biasT = np.ascontiguousarray(rb[bucket].transpose(0, 2, 1)).reshape(128, 4 * 640)
    kc = np.zeros((128, KC_N), f)
    kc[:, KC["maskA"]:KC["maskA"] + 640] = np.where(cc - kl >= 0, 0.0, NEG)
    s_ = np.arange(128)[:, None]
    l_ = np.arange(128)[None, :]
    kc[:, KC["tri"]:KC["tri"] + 128] = (s_ <= l_).astype(f)
    kc[:, KC["maskT"]:KC["maskT"] + 128] = np.where(s_ <= l_, LNK, NEG)
    kc[:, KC["maskL"]:KC["maskL"] + 128] = np.where(l_ <= s_, 0.0, NEG)
    kc[:, KC["ident"]:KC["ident"] + 128] = np.eye(128, dtype=f)
    kc[:, KC["eps"]] = EPS
    kc[:, KC["lnk"]] = LNK
    sh = {"cs": cs, "kc": kc, "rows": rows, "biasT": biasT.astype(f)}
    for nm in ("w_in", "w_br_m", "w_br_d", "w_out", "w_up", "w_down", "w_ple_gate", "w_ple"):
        sh[nm] = np.ascontiguousarray(np.asarray(inp[nm], f)[0])
    return sh


def make_in_maps(inp, cores):
    sh = _prep_shared(inp)
    x = np.asarray(inp["x"], np.float32)
    p = np.asarray(inp["p"], np.float32)[0]
    maps = []
    for b in cores:
        m = dict(sh)
        m["xT"] = np.ascontiguousarray(x[b].T)
        m["pT"] = np.ascontiguousarray(p[b].T)
        maps.append(m)
    return maps


_NC_CACHE = {}


def kernel(**inputs):
    if "nc" not in _NC_CACHE:
        _NC_CACHE["nc"] = build_nc()[0]
    nc = _NC_CACHE["nc"]
    in_maps = make_in_maps(inputs, list(range(8)))
    res = run_bass_kernel_spmd(nc, in_maps, core_ids=list(range(8)))
    out = np.stack([np.ascontiguousarray(np.asarray(r["outT"], np.float32).T) for r in res.results], axis=0)
    return out.astype(np.float32)
```
